# Optimizing a Trainium2 kernel written in Bass

```python
import math
import jax, jax.numpy as jnp
from jax import lax
import numpy as np

D_MODEL = 1024
BATCH = 8
SEQ = 4096
DEPTH = 2

GRID_W = 64
CTX_LEN = 256
N_MIXERS = 2
EPS = 1e-6
S5_GC = 16
S5_G = D_MODEL // S5_GC
S5_P = 64
MLA_H = 16
MLA_NOPE = 64
MLA_ROPE = 32
MLA_V = 64
MLA_QL = D_MODEL // 2
MLA_KVL = D_MODEL // 4
MLA_QK = MLA_NOPE + MLA_ROPE
ROPE_BASE = 10000.0
Q_BLOCK = 128
FFN_HIDDEN = ((8 * D_MODEL + 3 * 256 - 1) // (3 * 256)) * 256

kernel_name = 'hybrid_s5_mla_adaln_prefix_trunk'


def rmsnorm(x, g):
    xf = x.astype(jnp.float32)
    y = xf * lax.rsqrt(jnp.mean(xf * xf, axis=-1, keepdims=True) + EPS)
    return (y * g.astype(jnp.float32)).astype(x.dtype)


def modnorm(x, g, shift, scale):
    return rmsnorm(x, g) * (1 + scale[:, None, :]) + shift[:, None, :]


def swiglu(h, w_in, w_out):
    gu = h @ w_in
    g, u = gu[..., :FFN_HIDDEN], gu[..., FFN_HIDDEN:]
    return (jax.nn.silu(g) * u) @ w_out


def s5_discretize(A_re, A_im, log_step, B_re, B_im):
    lr = -jnp.abs(A_re.astype(jnp.float32))
    li = A_im.astype(jnp.float32)
    dt = jnp.exp(log_step.astype(jnp.float32))[:, None]
    mag = jnp.exp(lr * dt)
    ar = mag * jnp.cos(li * dt)
    ai = mag * jnp.sin(li * dt)
    den = lr * lr + li * li
    cr = ((ar - 1) * lr + ai * li) / den
    ci = (ai * lr - (ar - 1) * li) / den
    Br = B_re.astype(jnp.float32)
    Bi = B_im.astype(jnp.float32)
    bbr = cr[..., None] * Br - ci[..., None] * Bi
    bbi = cr[..., None] * Bi + ci[..., None] * Br
    return ar, ai, bbr, bbi


def s5_combine(e1, e2):
    a1r, a1i, b1r, b1i = e1
    a2r, a2i, b2r, b2i = e2
    ar = a1r * a2r - a1i * a2i
    ai = a1r * a2i + a1i * a2r
    br = a2r * b1r - a2i * b1i + b2r
    bi = a2r * b1i + a2i * b1r + b2i
    return ar, ai, br, bi


def s5_direction(u, h0, ar, ai, bbr, bbi, C_re, C_im, reverse, want_y):
    Bn, L, D = u.shape
    ug = u.reshape(Bn, L, S5_G, S5_GC).swapaxes(0, 1)
    br = jnp.einsum('lbgc,gpc->lbgp', ug, bbr)
    bi = jnp.einsum('lbgc,gpc->lbgp', ug, bbi)
    if h0 is not None:
        idx = L - 1 if reverse else 0
        h0r, h0i = h0
        br = br.at[idx].add(ar * h0r - ai * h0i)
        bi = bi.at[idx].add(ar * h0i + ai * h0r)
    a_r = jnp.broadcast_to(ar, (L, 1, S5_G, S5_P))
    a_i = jnp.broadcast_to(ai, (L, 1, S5_G, S5_P))
    _, _, hr, hi = lax.associative_scan(s5_combine, (a_r, a_i, br, bi), reverse=reverse, axis=0)
    end = 0 if reverse else L - 1
    h_end = (hr[end], hi[end])
    y = None
    if want_y:
        y = (jnp.einsum('lbgp,gcp->blgc', hr, C_re.astype(jnp.float32))
             - jnp.einsum('lbgp,gcp->blgc', hi, C_im.astype(jnp.float32))).reshape(Bn, L, D)
    return y, h_end


def s5_mixer(hx, hc, w_in, A_re, A_im, log_step, B_re, B_im, C_re, C_im, D_skip, w_glu, w_out, ctx_out):
    dt = hx.dtype
    ux = (hx @ w_in).astype(jnp.float32)
    uc = (hc @ w_in).astype(jnp.float32)
    Dk = D_skip.astype(jnp.float32)
    yx = Dk * ux
    yc = Dk * uc if ctx_out else None
    for d in range(2):
        rev = d == 1
        ar, ai, bbr, bbi = s5_discretize(A_re[d], A_im[d], log_step[d], B_re[d], B_im[d])
        yc_d, hc_end = s5_direction(uc, None, ar, ai, bbr, bbi, C_re[d], C_im[d], rev, ctx_out)
        yx_d, _ = s5_direction(ux, hc_end, ar, ai, bbr, bbi, C_re[d], C_im[d], rev, True)
        yx = yx + yx_d
        if ctx_out:
            yc = yc + yc_d

    def glu_out(y):
        z = jax.nn.gelu(y).astype(dt) @ w_glu
        v, g = z[..., :D_MODEL], z[..., D_MODEL:]
        return (v * jax.nn.sigmoid(g)) @ w_out

    return glu_out(yx), (glu_out(yc) if ctx_out else None)


def axial_rope_tables(L):
    rows = L // GRID_W
    row = jnp.repeat(jnp.arange(rows, dtype=jnp.float32), GRID_W)
    col = jnp.tile(jnp.arange(GRID_W, dtype=jnp.float32), rows)
    axis_dim = MLA_ROPE // 2
    inv_freq = ROPE_BASE ** (-jnp.arange(0, axis_dim, 2, dtype=jnp.float32) / axis_dim)
    ang = jnp.stack([row[:, None] * inv_freq, col[:, None] * inv_freq], axis=1)
    return jnp.cos(ang), jnp.sin(ang)


def apply_axial_rope(t, cos, sin):
    nope, pe = t[..., :MLA_NOPE], t[..., MLA_NOPE:]
    pe = pe.reshape(*pe.shape[:-1], 2, 2, MLA_ROPE // 4)
    x1, x2 = pe[..., 0, :], pe[..., 1, :]
    c = cos[None, :, None].astype(t.dtype)
    s = sin[None, :, None].astype(t.dtype)
    rot = jnp.stack([x1 * c - x2 * s, x2 * c + x1 * s], axis=-2).reshape(*t.shape[:-1], MLA_ROPE)
    return jnp.concatenate([nope, rot], axis=-1)


def mla_kv(proj, g_kva, w_kv_b, g_k):
    Bn, L, _ = proj.shape
    c_kv = proj[..., MLA_QL:MLA_QL + MLA_KVL]
    k_pe = proj[..., MLA_QL + MLA_KVL:]
    kv = (rmsnorm(c_kv, g_kva) @ w_kv_b).reshape(Bn, L, MLA_H, MLA_NOPE + MLA_V)
    k_nope, v = kv[..., :MLA_NOPE], kv[..., MLA_NOPE:]
    k = jnp.concatenate([k_nope, jnp.broadcast_to(k_pe[:, :, None, :], (Bn, L, MLA_H, MLA_ROPE))], axis=-1)
    return rmsnorm(k, g_k), v


def mla_q(proj, g_qa, w_q_b, g_q):
    Bn, L, _ = proj.shape
    q = (rmsnorm(proj[..., :MLA_QL], g_qa) @ w_q_b).reshape(Bn, L, MLA_H, MLA_QK)
    return rmsnorm(q, g_q)


def softmax_attend(q, k, v):
    s = jnp.einsum('bqhd,bkhd->bhqk', q, k).astype(jnp.float32) * (MLA_QK ** -0.5)
    p = jax.nn.softmax(s, axis=-1)
    return jnp.einsum('bhqk,bkhd->bqhd', p.astype(v.dtype), v)


def mla_mixer(hx, hc, w_in, g_qa, g_kva, w_q_b, w_kv_b, g_q, g_k, w_o, ctx_out):
    Bn, L, _ = hx.shape
    px = hx @ w_in
    pc = hc @ w_in
    cos, sin = axial_rope_tables(L)
    qx = apply_axial_rope(mla_q(px, g_qa, w_q_b, g_q), cos, sin)
    kx, vx = mla_kv(px, g_kva, w_kv_b, g_k)
    kx = apply_axial_rope(kx, cos, sin)
    kc, vc = mla_kv(pc, g_kva, w_kv_b, g_k)
    k_all = jnp.concatenate([kc, kx], axis=1)
    v_all = jnp.concatenate([vc, vx], axis=1)
    nblk = L // Q_BLOCK
    qb = qx.reshape(Bn, nblk, Q_BLOCK, MLA_H, MLA_QK).swapaxes(0, 1)
    ob = lax.map(lambda qq: softmax_attend(qq, k_all, v_all), qb)
    ox = ob.swapaxes(0, 1).reshape(Bn, L, MLA_H * MLA_V) @ w_o
    oc = None
    if ctx_out:
        qc = mla_q(pc, g_qa, w_q_b, g_q)
        oc = softmax_attend(qc, kc, vc).reshape(hc.shape[0], hc.shape[1], MLA_H * MLA_V) @ w_o
    return ox, oc


def setup_inputs(seed: int = 0) -> dict:
    key = jax.random.key(seed)
    ks = iter(jax.random.split(key, 64))
    f32 = jnp.float32
    D = D_MODEL
    NA = (DEPTH + 1) // 2
    NB = DEPTH // 2

    def nrm(shape, scale):
        return jax.random.normal(next(ks), shape, f32) * scale

    n = jnp.arange(S5_P, dtype=f32)
    return {
        'x': nrm((BATCH, SEQ, D), 1.0),
        'c': nrm((BATCH, D), 1.0),
        'ctx': nrm((BATCH, CTX_LEN, D), 1.0),
        'c_ctx': nrm((D,), 1.0),
        'mod_w': nrm((DEPTH, D, 6 * D), 0.5 * D ** -0.5),
        'mod_b': nrm((DEPTH, 6 * D), 0.02),
        'norm_mix': 1.0 + nrm((DEPTH, D), 0.02),
        'norm_ffn': 1.0 + nrm((DEPTH, D), 0.02),
        'ffn_w_in': nrm((DEPTH, D, 2 * FFN_HIDDEN), D ** -0.5),
        'ffn_w_out': nrm((DEPTH, FFN_HIDDEN, D), FFN_HIDDEN ** -0.5),
        's5_w_in': nrm((NA, D, D), D ** -0.5),
        's5_A_re': -0.5 + nrm((NA, 2, S5_G, S5_P), 0.01),
        's5_A_im': math.pi * n + nrm((NA, 2, S5_G, S5_P), 0.01),
        's5_log_step': jax.random.uniform(next(ks), (NA, 2, S5_G), f32, math.log(1e-3), math.log(1e-1)),
        's5_B_re': nrm((NA, 2, S5_G, S5_P, S5_GC), (2 * S5_GC) ** -0.5),
        's5_B_im': nrm((NA, 2, S5_G, S5_P, S5_GC), (2 * S5_GC) ** -0.5),
        's5_C_re': nrm((NA, 2, S5_G, S5_GC, S5_P), S5_P ** -0.5),
        's5_C_im': nrm((NA, 2, S5_G, S5_GC, S5_P), S5_P ** -0.5),
        's5_D': nrm((NA, D), 1.0),
        's5_w_glu': nrm((NA, D, 2 * D), D ** -0.5),
        's5_w_out': nrm((NA, D, D), D ** -0.5),
        'mla_w_in': nrm((NB, D, MLA_QL + MLA_KVL + MLA_ROPE), D ** -0.5),
        'mla_q_a_norm': 1.0 + nrm((NB, MLA_QL), 0.02),
        'mla_kv_a_norm': 1.0 + nrm((NB, MLA_KVL), 0.02),
        'mla_w_q_b': nrm((NB, MLA_QL, MLA_H * MLA_QK), MLA_QL ** -0.5),
        'mla_w_kv_b': nrm((NB, MLA_KVL, MLA_H * (MLA_NOPE + MLA_V)), MLA_KVL ** -0.5),
        'mla_q_norm': 1.0 + nrm((NB, MLA_QK), 0.02),
        'mla_k_norm': 1.0 + nrm((NB, MLA_QK), 0.02),
        'mla_w_o': nrm((NB, MLA_H * MLA_V, D), (MLA_H * MLA_V) ** -0.5),
    }


def reference(x, c, ctx, c_ctx, mod_w, mod_b, norm_mix, norm_ffn, ffn_w_in, ffn_w_out,
              s5_w_in, s5_A_re, s5_A_im, s5_log_step, s5_B_re, s5_B_im, s5_C_re, s5_C_im, s5_D,
              s5_w_glu, s5_w_out,
              mla_w_in, mla_q_a_norm, mla_kv_a_norm, mla_w_q_b, mla_w_kv_b, mla_q_norm, mla_k_norm, mla_w_o):
    D = D_MODEL
    for i in range(DEPTH):
        last = i == DEPTH - 1
        j = i // N_MIXERS
        mod = jax.nn.silu(c) @ mod_w[i] + mod_b[i]
        mod_c = jax.nn.silu(c_ctx)[None, :] @ mod_w[i] + mod_b[i]
        sh_m, sc_m, gt_m, sh_f, sc_f, gt_f = [mod[:, k * D:(k + 1) * D] for k in range(6)]
        csh_m, csc_m, cgt_m, csh_f, csc_f, cgt_f = [mod_c[:, k * D:(k + 1) * D] for k in range(6)]
        hx = modnorm(x, norm_mix[i], sh_m, sc_m)
        hc = modnorm(ctx, norm_mix[i], csh_m, csc_m)
        if i % N_MIXERS == 0:
            ox, oc = s5_mixer(hx, hc, s5_w_in[j], s5_A_re[j], s5_A_im[j], s5_log_step[j],
                              s5_B_re[j], s5_B_im[j], s5_C_re[j], s5_C_im[j], s5_D[j],
                              s5_w_glu[j], s5_w_out[j], not last)
        else:
            ox, oc = mla_mixer(hx, hc, mla_w_in[j], mla_q_a_norm[j], mla_kv_a_norm[j], mla_w_q_b[j],
                               mla_w_kv_b[j], mla_q_norm[j], mla_k_norm[j], mla_w_o[j], not last)
        x = x + gt_m[:, None, :] * ox.astype(x.dtype)
        x = x + gt_f[:, None, :] * swiglu(modnorm(x, norm_ffn[i], sh_f, sc_f), ffn_w_in[i], ffn_w_out[i])
        if not last:
            ctx = ctx + cgt_m[:, None, :] * oc.astype(ctx.dtype)
            ctx = ctx + cgt_f[:, None, :] * swiglu(modnorm(ctx, norm_ffn[i], csh_f, csc_f), ffn_w_in[i], ffn_w_out[i])
    return x
```

```python
import math
import numpy as np
from contextlib import ExitStack
import concourse.bass as bass
import concourse.mybir as mybir
from concourse.bass_utils import run_bass_kernel_spmd

F32 = mybir.dt.float32
BF16 = mybir.dt.bfloat16
I32 = mybir.dt.int32
AF = mybir.ActivationFunctionType
ALU = mybir.AluOpType
AX = mybir.AxisListType

D = 1024
SEQ = 4096
CTX = 256
NTOK = SEQ + CTX
NT = NTOK // 128
FH = 2816
NCH = NTOK // 8
EPS = 1e-6
PI = math.pi


class K:
    def __init__(self, nc, st, nds=24):
        self.nc = nc
        self.st = st
        self.E = {'pe': nc.tensor, 'act': nc.scalar, 'dve': nc.vector, 'pool': nc.gpsimd, 'sp': nc.sync}
        self.esem = {e: st.enter_context(nc.semaphore('s_' + e)) for e in ['pe', 'act', 'dve', 'pool']}
        self.ecnt = dict.fromkeys(self.esem, 0)
        self.NDS = nds
        self.dsem = [st.enter_context(nc.semaphore(f'dq{i}')) for i in range(nds)]
        self.dcnt = [0] * nds
        self.dnext = 0
        self.seen = {}
        self.lastw = {}
        self.readers = {}
        self.nops = 0
        self.ev = {}

    def _sem(self, k):
        return self.esem[k] if isinstance(k, str) else self.dsem[k[1]]

    def _wait(self, eng, deps):
        need = {}
        for (k, v) in deps:
            if v > need.get(k, 0):
                need[k] = v
        for k, v in need.items():
            if eng == 'pe' and k == 'pe':
                continue
            if self.seen.get((eng, k), 0) >= v:
                continue
            self.E[eng].wait_ge(self._sem(k), v)
            self.ev.setdefault(eng, []).append(('w', k, v))
            self.seen[(eng, k)] = v

    def _deps(self, reads, writes):
        deps = []
        for b in reads:
            if b in self.lastw:
                deps.append(self.lastw[b])
        for b in writes:
            if b in self.lastw:
                deps.append(self.lastw[b])
            deps += list(self.readers.get(b, {}).items())
        return deps

    def _record(self, tok, reads, writes):
        for b in reads:
            r = self.readers.setdefault(b, {})
            if tok[1] > r.get(tok[0], 0):
                r[tok[0]] = tok[1]
        for b in writes:
            self.lastw[b] = tok
            self.readers[b] = {}

    def op(self, eng, fn, reads=(), writes=(), inc=True):
        self._wait(eng, self._deps(reads, writes))
        ins = fn(self.E[eng])
        tok = (eng, self.ecnt[eng] + 1)
        if inc:
            self.ecnt[eng] += 1
            ins.then_inc(self.esem[eng], 1)
            self.ev.setdefault(eng, []).append(('i', eng, 1, self.nops))
        self._record(tok, reads, writes)
        self.nops += 1
        return ins

    def dma(self, out, in_, reads=(), writes=(), q='sp', slow=False):
        i = self.dnext
        self.dnext = (i + 1) % self.NDS
        deps = self._deps(reads, writes)
        if self.dcnt[i] > 0:
            deps.append((('d', i), self.dcnt[i]))
        self._wait(q, deps)
        self.dcnt[i] += 16
        (self.E[q].dma_start(out=out, in_=in_, allow_slow_non_contiguous=True) if slow else self.E[q].dma_start(out=out, in_=in_)).then_inc(self.dsem[i], 16)
        self._record((('d', i), self.dcnt[i]), reads, writes)
        self.ev.setdefault(q, []).append(('i', ('d', i), 16, self.nops))
        self.nops += 1

    def barrier(self):
        deps = [(e, self.ecnt[e]) for e in self.esem if self.ecnt[e] > 0]
        deps += [(('d', i), self.dcnt[i]) for i in range(self.NDS) if self.dcnt[i] > 0]
        for e in ['pe', 'act', 'dve', 'pool', 'sp']:
            self._wait(e, [d for d in deps if d[0] != e])

    def finish(self):
        self.barrier()

    def sb(self, name, shape, dt, st=None):
        self.uid = getattr(self, 'uid', 0) + 1
        return (st or self.st).enter_context(self.nc.sbuf_tensor(f'{name}_u{self.uid}', shape, dt))

    def ps(self, name, shape, dt, st=None):
        self.uid = getattr(self, 'uid', 0) + 1
        return (st or self.st).enter_context(self.nc.psum_tensor(f'{name}_u{self.uid}', shape, dt))


class Ctx:
    pass


def build(upto='all', debug=False):
    nc = bass.Bass("TRN2", target_bir_lowering=False)
    G = Ctx()
    G.nc = nc
    I = {}

    def inp(name, shape):
        I[name] = nc.dram_tensor(name, list(shape), F32, kind="ExternalInput").ap()

    inp('x', [SEQ, D]); inp('c', [D]); inp('ctx', [CTX, D]); inp('c_ctx', [D])
    inp('mod_w', [2, D, 6 * D]); inp('mod_b', [2, 6 * D]); inp('norm_mix', [2, D]); inp('norm_ffn', [2, D])
    inp('ffn_w_in', [2, D, 2 * FH]); inp('ffn_w_out', [2, FH, D])
    inp('s5_w_in', [1, D, D]); inp('s5_A_re', [1, 2, 64, 64]); inp('s5_A_im', [1, 2, 64, 64])
    inp('s5_log_step', [1, 2, 64]); inp('s5_B_re', [1, 2, 64, 64, 16]); inp('s5_B_im', [1, 2, 64, 64, 16])
    inp('s5_C_re', [1, 2, 64, 16, 64]); inp('s5_C_im', [1, 2, 64, 16, 64]); inp('s5_D', [1, D])
    inp('s5_w_glu', [1, D, 2 * D]); inp('s5_w_out', [1, D, D])
    inp('mla_w_in', [1, D, 800]); inp('mla_q_a_norm', [1, 512]); inp('mla_kv_a_norm', [1, 256])
    inp('mla_w_q_b', [1, 512, 1536]); inp('mla_w_kv_b', [1, 256, 2048]); inp('mla_q_norm', [1, 96])
    inp('mla_k_norm', [1, 96]); inp('mla_w_o', [1, D, D])
    inp('rope_cs', [SEQ, 32])
    out_ap = nc.dram_tensor('out', [SEQ, D], F32, kind="ExternalOutput").ap()

    dbgset = set(debug) if debug else set()

    def scratch(name, shape, dt=F32):
        return nc.dram_tensor(name, list(shape), dt, kind=("ExternalOutput" if name in dbgset else "Internal")).ap()

    S = {}
    S['u'] = scratch('u_scr', [NTOK, D])
    S['y'] = scratch('y_scr', [NTOK, D])
    S['o'] = scratch('o_scr', [NTOK, D], BF16)
    S['x1'] = scratch('x1_scr', [NTOK, D])
    S['hid'] = scratch('hid_scr', [NTOK, FH], BF16)
    S['x2'] = scratch('x2_scr', [NTOK, D])
    S['U'] = scratch('U_scr', [128, 64, NCH], BF16)
    S['H0'] = scratch('H0_scr', [64, 2, 64, NCH])
    S['H1'] = scratch('H1_scr', [64, 2, 64, NCH])
    S['P'] = scratch('P_scr', [NTOK, 800])
    S['q'] = scratch('q_scr', [SEQ, 1536], BF16)
    S['k'] = scratch('k_scr', [NTOK, 1536], BF16)
    S['v'] = scratch('v_scr', [NTOK, 16, 65], BF16)
    S['att'] = scratch('att_scr', [SEQ, D], BF16)
    S['x3'] = scratch('x3_scr', [SEQ, D])

    with ExitStack() as st:
        k = K(nc, st)
        G.k = k
        identf = k.sb('identf', [128, 128], F32)
        ident = k.sb('ident', [128, 128], BF16)
        k.op('pool', lambda e: e.memset(identf[:], 1.0), writes=['identf'])
        k.op('pool', lambda e: e.affine_select(identf[:], identf[:], [[-1, 128]], ALU.is_equal, 0.0, base=0, channel_multiplier=1),
             reads=['identf'], writes=['identf'])
        k.op('dve', lambda e: e.tensor_copy(ident[:], identf[:]), reads=['identf'], writes=['ident'])
        G.ident = ident
        G.identf = identf
        G.mhalf = k.sb('mhalf', [128, 2], F32)
        k.op('pool', lambda e: e.memset(G.mhalf[:], -0.5), writes=['mhalf'])

        def stream_src(name):
            return lambda t: S[name][t * 128:(t + 1) * 128, :]

        def x_in(t):
            return I['ctx'][t * 128:(t + 1) * 128, :] if t < 2 else I['x'][(t - 2) * 128:(t - 1) * 128, :]

        def load_w(wsb, W, Kdim, key, q='pool'):
            for kc in range(Kdim // 128):
                k.dma(wsb[:, kc, :], W[kc * 128:(kc + 1) * 128, :], writes=[(key, kc)], q=q)

        def linear_phase(pname, tiles, Kdim, N, wsb, wkey, load_fn, prologue, epilogue, src_dt=F32, ybufs=2):
            KC = Kdim // 128
            with ExitStack() as ps:
                xt = [k.sb(f'{pname}_xt{i}', [128, Kdim], src_dt, ps) for i in range(3)]
                xb = [k.sb(f'{pname}_xb{i}', [128, Kdim], BF16, ps) for i in range(3)] if prologue else None
                xT = [k.sb(f'{pname}_xT{i}', [128, KC, 128], BF16, ps) for i in range(2)]
                yt = [k.sb(f'{pname}_yt{i}', [128, N], F32, ps) for i in range(ybufs)]
                pT = [k.ps(f'{pname}_pT{i}', [128, 8, 128], BF16, ps) for i in range(2)]
                pM = [k.ps(f'{pname}_pM{i}', [128, 512], F32, ps) for i in range(4)]
                cnts = {'pt': 0, 'pm': 0, 'ev': 0}
                info = {}

                def stage0(it, t):
                    b3 = it % 3
                    ikey = (pname, 'xt', b3)
                    load_fn(t, xt[b3], ikey)
                    if prologue:
                        prologue(t, xt[b3], xb[b3], ikey, (pname, 'xb', b3))

                def stage1(it, t):
                    b = it % 2
                    b3 = it % 3
                    if prologue:
                        src, skey = xb[b3], (pname, 'xb', b3)
                    else:
                        src, skey = xt[b3], (pname, 'xt', b3)
                    tkey = (pname, 'xT', b)
                    for k0 in range(0, KC, 8):
                        kn = min(8, KC - k0)
                        pb = cnts['pt'] % 2
                        cnts['pt'] += 1
                        for j in range(kn):
                            k.op('pe', lambda e, j=j: e.transpose(pT[pb][:, j, :], src[:, (k0 + j) * 128:(k0 + j + 1) * 128], ident[:]),
                                 reads=[skey, 'ident'], writes=[(pname, 'pT', pb)], inc=(j == kn - 1))
                        eng = 'act' if (cnts['ev'] % 2 == 0) else 'dve'
                        cnts['ev'] += 1
                        if eng == 'act':
                            k.op('act', lambda e: e.copy(xT[b][:, k0:k0 + kn, :], pT[pb][:, 0:kn, :]),
                                 reads=[(pname, 'pT', pb)], writes=[(tkey, k0)])
                        else:
                            k.op('dve', lambda e: e.tensor_copy(xT[b][:, k0:k0 + kn, :], pT[pb][:, 0:kn, :]),
                                 reads=[(pname, 'pT', pb)], writes=[(tkey, k0)])

                def stage2(it, t):
                    b = it % 2
                    tkey = (pname, 'xT', b)
                    yb = it % ybufs
                    ykeys = []
                    for n0 in range(0, N, 512):
                        w = min(512, N - n0)
                        pm = cnts['pm'] % 4
                        cnts['pm'] += 1
                        for kc in range(KC):
                            k.op('pe', lambda e, kc=kc: e.matmul(pM[pm][:, 0:w], xT[b][:, kc, :], wsb[:, kc, n0:n0 + w],
                                                                start=(kc == 0), stop=(kc == KC - 1)),
                                 reads=[(tkey, (kc // 8) * 8), (wkey, kc)], writes=[(pname, 'pM', pm)], inc=(kc == KC - 1))
                        yk = (pname, 'yt', yb, n0)
                        ykeys.append(yk)
                        eng = 'act' if (cnts['ev'] % 2 == 0) else 'dve'
                        cnts['ev'] += 1
                        if eng == 'act':
                            k.op('act', lambda e: e.copy(yt[yb][:, n0:n0 + w], pM[pm][:, 0:w]), reads=[(pname, 'pM', pm)], writes=[yk])
                        else:
                            k.op('dve', lambda e: e.tensor_copy(yt[yb][:, n0:n0 + w], pM[pm][:, 0:w]), reads=[(pname, 'pM', pm)], writes=[yk])
                    epilogue(t, yt[yb], ykeys)

                import os
                PIPE = int(os.environ.get('PIPE', '2'))
                if PIPE == 2:
                    stage0(0, tiles[0])
                    if len(tiles) > 1:
                        stage0(1, tiles[1])
                    stage1(0, tiles[0])
                    for it, t in enumerate(tiles):
                        if it + 2 < len(tiles):
                            stage0(it + 2, tiles[it + 2])
                        if it + 1 < len(tiles):
                            stage1(it + 1, tiles[it + 1])
                        stage2(it, t)
                elif PIPE == 1:
                    stage0(0, tiles[0])
                    stage1(0, tiles[0])
                    for it, t in enumerate(tiles):
                        if it + 1 < len(tiles):
                            stage0(it + 1, tiles[it + 1])
                            stage1(it + 1, tiles[it + 1])
                        stage2(it, t)
                else:
                    for it, t in enumerate(tiles):
                        stage0(it, t)
                        stage1(it, t)
                        stage2(it, t)
            k.barrier()

        def rms_rstd(src_ap, n, ss, key_in, pname, sq):
            k.op('dve', lambda e: e.tensor_tensor(sq, src_ap, src_ap, ALU.mult), reads=[key_in], writes=[(pname, 'sq')])
            k.op('dve', lambda e: e.reduce_sum(ss[:, 0:1], sq, AX.X), reads=[(pname, 'sq')], writes=[(pname, 'ss0')])
            k.op('act', lambda e: e.activation(ss[:, 1:2], ss[:, 0:1], AF.Sqrt, bias=EPS, scale=1.0 / n),
                 reads=[(pname, 'ss0')], writes=[(pname, 'ss1')])
            k.op('dve', lambda e: e.reciprocal(ss[:, 1:2], ss[:, 1:2]), reads=[(pname, 'ss1')], writes=[(pname, 'ss1')])

        def make_modnorm(pname, ia, ish, stk):
            sq = k.sb(pname + '_sq', [128, D], F32, stk)
            ss = k.sb(pname + '_ss', [128, 2], F32, stk)

            def pro(t, xt, xb, ikey, okey):
                mv = G.modv[1] if t < 2 else G.modv[0]
                rms_rstd(xt[:], D, ss, ikey, pname, sq[:])
                k.op('dve', lambda e: e.scalar_tensor_tensor(sq[:], xt[:], ss[:, 1:2], mv[:, ia, :], ALU.mult, ALU.mult),
                     reads=[ikey, (pname, 'ss1'), 'modv'], writes=[(pname, 'sq')])
                k.op('pool', lambda e: e.tensor_tensor(xb[:], sq[:], mv[:, ish, :], ALU.add),
                     reads=[(pname, 'sq'), 'modv'], writes=[okey])
            return pro

        def make_residual_epi(pname, res_fn, ig, dst_fn, stk, dkey=None, rkey=None):
            rt = [k.sb(f'{pname}_rt{i}', [128, D], F32, stk) for i in range(2)]
            cnt = [0]

            def epi(t, yt, ykeys):
                b = cnt[0] % 2
                cnt[0] += 1
                mv = G.modv[1] if t < 2 else G.modv[0]
                rk = (pname, 'rt', b)
                k.dma(rt[b][:], res_fn(t), reads=([(rkey, t)] if rkey else []), writes=[rk])
                k.op('dve', lambda e: e.tensor_tensor(yt[:, 0:D], yt[:, 0:D], mv[:, ig, :], ALU.mult), reads=ykeys + ['modv'], writes=ykeys)
                k.op('pool', lambda e: e.tensor_tensor(rt[b][:], rt[b][:], yt[:, 0:D], ALU.add), reads=ykeys + [rk], writes=[rk])
                k.dma(dst_fn(t), rt[b][:], reads=[rk], writes=([(dkey, t)] if dkey else []))
            return epi

        def mod_phase(l):
            with ExitStack() as ps:
                wsb = k.sb('modw', [128, 8, 6 * D], BF16, ps)
                load_w(wsb, I['mod_w'][l], D, 'modw')
                cc = k.sb('modcc', [128, 2, 8], F32, ps)
                k.dma(cc[:, 0, :], I['c'].rearrange("(kc p) -> p kc", p=128), writes=['modcc'], slow=True)
                k.dma(cc[:, 1, :], I['c_ctx'].rearrange("(kc p) -> p kc", p=128), writes=['modcc'], slow=True)
                k.op('act', lambda e: e.activation(cc[:], cc[:], AF.Silu), reads=['modcc'], writes=['modcc'])
                lh = k.sb('modlh', [128, 2, 8, 128], BF16, ps)
                for who in range(2):
                    for kc in range(8):
                        k.op('dve', lambda e, who=who, kc=kc: e.tensor_copy(lh[:, who, kc, :], cc[:, who, kc:kc + 1].to_broadcast([128, 128])),
                             reads=['modcc'], writes=['modlh'])
                bb = k.sb('modbb', [128, 6 * D], F32, ps)
                k.dma(bb[:], I['mod_b'][l].partition_broadcast(128), writes=['modbb'])
                gm = k.sb('modg', [128, 2, D], F32, ps)
                k.dma(gm[:, 0, :], I['norm_mix'][l].partition_broadcast(128), writes=['modg'])
                k.dma(gm[:, 1, :], I['norm_ffn'][l].partition_broadcast(128), writes=['modg'])
                pM = [k.ps(f'mod_pM{i}', [128, 512], F32, ps) for i in range(2)]
                cp = 0
                for who in range(2):
                    mv = G.modv[who]
                    for j in range(12):
                        pm = cp % 2
                        cp += 1
                        for kc in range(8):
                            k.op('pe', lambda e, kc=kc: e.matmul(pM[pm][:], lh[:, who, kc, :], wsb[:, kc, j * 512:(j + 1) * 512],
                                                                start=(kc == 0), stop=(kc == 7)),
                                 reads=['modlh', ('modw', kc)], writes=[('modpm', pm)], inc=(kc == 7))
                        mvf = mv[:].rearrange("p a d -> p (a d)")
                        k.op('dve', lambda e: e.tensor_tensor(mvf[:, j * 512:(j + 1) * 512], pM[pm][:], bb[:, j * 512:(j + 1) * 512], ALU.add),
                             reads=[('modpm', pm), 'modbb'], writes=['modv'])
                    k.op('dve', lambda e: e.scalar_tensor_tensor(mv[:, 1, :], mv[:, 1, :], 1.0, gm[:, 0, :], ALU.add, ALU.mult),
                         reads=['modv', 'modg'], writes=['modv'])
                    k.op('dve', lambda e: e.scalar_tensor_tensor(mv[:, 4, :], mv[:, 4, :], 1.0, gm[:, 1, :], ALU.add, ALU.mult),
                         reads=['modv', 'modg'], writes=['modv'])
            k.barrier()

        def alloc_modv(stk, tag):
            G.modv = [k.sb(f'modv{tag}_{i}', [128, 6, D], F32, stk) for i in range(2)]

        def simple_load(src_fn, q='sp'):
            def f(t, tile, key):
                k.dma(tile[:], src_fn(t), writes=[key], q=q)
            return f

        def ffn(l, tiles, src_fn, dst_fn, skey, dkey):
            with ExitStack() as ps:
                wsb = k.sb(f'ffw{l}', [128, 8, 2 * FH], BF16, ps)
                load_w(wsb, I['ffn_w_in'][l], D, f'ffw{l}')
                pro = make_modnorm(f'ffa{l}', 4, 3, ps)
                hb = [k.sb(f'ffa{l}_hb{i}', [128, FH], BF16, ps) for i in range(2)]
                cnt = [0]

                def epi(t, yt, ykeys):
                    b = cnt[0] % 2
                    cnt[0] += 1
                    k.op('act', lambda e: e.activation(yt[:, 0:FH], yt[:, 0:FH], AF.Silu), reads=ykeys, writes=ykeys)
                    k.op('dve', lambda e: e.tensor_tensor(hb[b][:], yt[:, 0:FH], yt[:, FH:2 * FH], ALU.mult), reads=ykeys, writes=[('ffhb', l, b)])
                    k.dma(S['hid'][t * 128:(t + 1) * 128, :], hb[b][:], reads=[('ffhb', l, b)], writes=[('hid', t)])
                def lds(t, tile, key):
                    k.dma(tile[:], src_fn(t), reads=[(skey, t)], writes=[key])
                linear_phase(f'ffa{l}', tiles, D, 2 * FH, wsb, f'ffw{l}', lds, pro, epi, ybufs=1)
            with ExitStack() as ps:
                wsb = k.sb(f'ffv{l}', [128, 22, D], BF16, ps)
                load_w(wsb, I['ffn_w_out'][l], FH, f'ffv{l}')
                epi = make_residual_epi(f'ffb{l}', src_fn, 5, dst_fn, ps, dkey=dkey, rkey=skey)

                def ld(t, tile, key):
                    k.dma(tile[:], S['hid'][t * 128:(t + 1) * 128, :], reads=[('hid', t)], writes=[key])
                linear_phase(f'ffb{l}', tiles, FH, D, wsb, f'ffv{l}', ld, None, epi, src_dt=BF16)

        ALLT = list(range(NT))
        XT = list(range(2, NT))
        with ExitStack() as m0:
            alloc_modv(m0, 'a')
            mod_phase(0)
            with ExitStack() as ps:
                wsb = k.sb('s5wi', [128, 8, D], BF16, ps)
                load_w(wsb, I['s5_w_in'][0], D, 's5wi')
                pro = make_modnorm('s5i', 1, 0, ps)

                def epi(t, yt, ykeys):
                    k.dma(S['u'][t * 128:(t + 1) * 128, :], yt[:], reads=ykeys, writes=[('u', t)])
                linear_phase('s5i', ALLT, D, D, wsb, 's5wi', simple_load(x_in), pro, epi)
        k.barrier()
        if upto == 'u':
            k.finish()
            return nc

        s5_scan(G, I, S)
        if upto == 'y':
            k.finish()
            return nc

        with ExitStack() as m0:
            alloc_modv(m0, 'b')
            mod_phase(0)
            with ExitStack() as ps:
                wsb = k.sb('s5wg', [128, 8, 2 * D], BF16, ps)
                load_w(wsb, I['s5_w_glu'][0], D, 's5wg')
                ob = [k.sb(f's5g_ob{i}', [128, D], BF16, ps) for i in range(2)]
                cnt = [0]

                def pro(t, xt, xb, ikey, okey):
                    k.op('act', lambda e: e.activation(xb[:], xt[:], AF.Gelu_apprx_tanh), reads=[ikey], writes=[okey])

                def epi(t, yt, ykeys):
                    b = cnt[0] % 2
                    cnt[0] += 1
                    k.op('act', lambda e: e.activation(yt[:, D:2 * D], yt[:, D:2 * D], AF.Sigmoid), reads=ykeys, writes=ykeys)
                    k.op('dve', lambda e: e.tensor_tensor(ob[b][:], yt[:, 0:D], yt[:, D:2 * D], ALU.mult), reads=ykeys, writes=[('s5ob', b)])
                    k.dma(S['o'][t * 128:(t + 1) * 128, :], ob[b][:], reads=[('s5ob', b)], writes=[('o', t)])

                def ldy(t, tile, key):
                    k.dma(tile[:], S['y'][t * 128:(t + 1) * 128, :], reads=['y_scr'], writes=[key])
                linear_phase('s5g', ALLT, D, 2 * D, wsb, 's5wg', ldy, pro, epi)
            with ExitStack() as ps:
                wsb = k.sb('s5wo', [128, 8, D], BF16, ps)
                load_w(wsb, I['s5_w_out'][0], D, 's5wo')

                def dst1(t):
                    return S['x1'][t * 128:(t + 1) * 128, :]
                epi = make_residual_epi('s5o', x_in, 2, dst1, ps, dkey='x1')

                def ldo(t, tile, key):
                    k.dma(tile[:], S['o'][t * 128:(t + 1) * 128, :], reads=[('o', t)], writes=[key])
                linear_phase('s5o', ALLT, D, D, wsb, 's5wo', ldo, None, epi, src_dt=BF16)
            if upto == 'x1':
                k.finish()
                return nc
            ffn(0, ALLT, stream_src('x1'), stream_src('x2'), 'x1', 'x2')
        k.barrier()
        if upto == 'x2':
            k.finish()
            return nc

        with ExitStack() as m1:
            alloc_modv(m1, 'c')
            mod_phase(1)
            mla(G, I, S, linear_phase, load_w, make_modnorm, make_residual_epi, simple_load, stream_src, rms_rstd)
            if upto == 'x3':
                k.finish()
                return nc
            ffn(1, XT, lambda t: S['x3'][(t - 2) * 128:(t - 1) * 128, :], lambda t: out_ap[(t - 2) * 128:(t - 1) * 128, :], 'x3', 'outk')
        k.finish()
    G.kk = k
    build.last_k = k
    return nc


def s5_scan(G, I, S):
    k = G.k
    ident, identf = G.ident, G.identf
    NB = 32
    NBLK = NCH // NB
    with ExitStack() as s5:
        PC = [[k.sb(f'PC{d}{r}', [64, 64, 9, 16], BF16, s5) for r in range(2)] for d in range(2)]
        Toep = k.sb('Toep', [128, 64, 128], BF16, s5)
        WinT = k.sb('WinT', [128, 64, 2, 2, 64], BF16, s5)
        COEF = [k.sb(f'COEF{d}', [64, 4, 64], F32, s5) for d in range(2)]
        with ExitStack() as gen:
            k.op('pool', lambda e: e.memset(Toep[:], 0.0), writes=['Toep'])
            pG = [k.ps(f'pG{i}', [128, 512], F32, gen) for i in range(2)]
            pGb = [k.ps(f'pGb{i}', [128, 1024], BF16, gen) for i in range(2)]
            Kb0 = k.sb('Kb0', [16, 64, 16], F32, gen)
            Dbc = k.sb('Dbc', [16, 64, 16], F32, gen)
            k.dma(Dbc[:].rearrange("p g c -> p (g c)"), I['s5_D'][0].partition_broadcast(16), writes=['Dbc'])
            cg = [0]
            for d in (1, 0):
                with ExitStack() as gd:
                    def gt(name, shape, dt=F32, stk=None):
                        return k.sb(f'g{d}_{name}', shape, dt, stk or gd)
                    lam = gt('lam', [64, 2, 64])
                    BBp = [gt(f'BBp{r}', [64, 64, 16]) for r in range(2)]
                    CCp = [gt(f'CCp{r}', [64, 64, 16]) for r in range(2)]
                    PW = gt('PW', [64, 9, 2, 64])
                    EB = gt('EB', [64, 2, 64, 8]); EC = gt('EC', [64, 2, 64, 9])
                    BBb = [gt(f'BBb{r}', [64, 64, 16], BF16) for r in range(2)]
                    gk = 'gsm'
                    with ExitStack() as g1:
                        Are = gt('Are', [64, 64], F32, g1); Aim = gt('Aim', [64, 64], F32, g1); ls = gt('ls', [64, 1], F32, g1)
                        Bre = gt('Bre', [64, 64, 16], F32, g1); Bim = gt('Bim', [64, 64, 16], F32, g1)
                        Cre = gt('Cre', [64, 16, 64], F32, g1); Cim = gt('Cim', [64, 16, 64], F32, g1)
                        k.dma(Are[:], I['s5_A_re'][0, d], writes=['Are'])
                        k.dma(Aim[:], I['s5_A_im'][0, d], writes=['Aim'])
                        k.dma(ls[:], I['s5_log_step'][0, d].rearrange("(g o) -> g o", o=1), writes=['ls'])
                        k.dma(Bre[:], I['s5_B_re'][0, d], writes=['Bre'])
                        k.dma(Bim[:], I['s5_B_im'][0, d], writes=['Bim'])
                        k.dma(Cre[:], I['s5_C_re'][0, d], writes=['Cre'])
                        k.dma(Cim[:], I['s5_C_im'][0, d], writes=['Cim'])
                        T = [gt(f't{i}', [64, 64], F32, g1) for i in range(10)]
                        Ti = gt('ti', [64, 64], I32, g1)

                        def dv(fn, rd=(), wr=()):
                            k.op('dve', fn, reads=[gk] + list(rd), writes=[gk] + list(wr))

                        def ac(fn, rd=(), wr=()):
                            k.op('act', fn, reads=[gk] + list(rd), writes=[gk] + list(wr))
                        lr, ang, mag, cs, sn, ar, ai, den, cr, ci = T
                        ac(lambda e: e.activation(ls[:], ls[:], AF.Exp), rd=['ls'])
                        dv(lambda e: e.tensor_scalar(mag[:], Are[:], -1.0, None, ALU.mult), rd=['Are'])
                        dv(lambda e: e.tensor_tensor(lr[:], Are[:], mag[:], ALU.min))
                        dv(lambda e: e.tensor_scalar(mag[:], lr[:], ls[:, 0:1], None, ALU.mult))
                        ac(lambda e: e.activation(mag[:], mag[:], AF.Exp))
                        dv(lambda e: e.tensor_scalar(ang[:], Aim[:], ls[:, 0:1], None, ALU.mult), rd=['Aim'])

                        def sin_of(dst, shift):
                            dv(lambda e: e.tensor_scalar(dst[:], ang[:], shift + 2 * PI, 1.0 / (2 * PI), ALU.add, ALU.mult))
                            dv(lambda e: e.tensor_copy(Ti[:], dst[:]))
                            dv(lambda e: e.tensor_copy(den[:], Ti[:]))
                            dv(lambda e: e.tensor_scalar(dst[:], ang[:], shift + 2 * PI, None, ALU.add))
                            dv(lambda e: e.scalar_tensor_tensor(dst[:], den[:], -2 * PI, dst[:], ALU.mult, ALU.add))
                            dv(lambda e: e.tensor_scalar(den[:], dst[:], PI, -2 * PI, ALU.is_gt, ALU.mult))
                            dv(lambda e: e.tensor_tensor(dst[:], dst[:], den[:], ALU.add))
                            dv(lambda e: e.tensor_scalar(den[:], dst[:], -PI, 2 * PI, ALU.is_lt, ALU.mult))
                            dv(lambda e: e.tensor_tensor(dst[:], dst[:], den[:], ALU.add))
                            ac(lambda e: e.activation(dst[:], dst[:], AF.Sin))
                        sin_of(cs, PI / 2)
                        sin_of(sn, 0.0)
                        dv(lambda e: e.tensor_tensor(ar[:], mag[:], cs[:], ALU.mult))
                        dv(lambda e: e.tensor_tensor(ai[:], mag[:], sn[:], ALU.mult))
                        dv(lambda e: e.tensor_tensor(den[:], lr[:], lr[:], ALU.mult))
                        dv(lambda e: e.tensor_tensor(cs[:], Aim[:], Aim[:], ALU.mult))
                        dv(lambda e: e.tensor_tensor(den[:], den[:], cs[:], ALU.add))
                        dv(lambda e: e.reciprocal(den[:], den[:]))
                        dv(lambda e: e.tensor_scalar(mag[:], ar[:], -1.0, None, ALU.add))
                        dv(lambda e: e.tensor_tensor(cs[:], mag[:], lr[:], ALU.mult))
                        dv(lambda e: e.tensor_tensor(sn[:], ai[:], Aim[:], ALU.mult))
                        dv(lambda e: e.tensor_tensor(cr[:], cs[:], sn[:], ALU.add))
                        dv(lambda e: e.tensor_tensor(cr[:], cr[:], den[:], ALU.mult))
                        dv(lambda e: e.tensor_tensor(cs[:], ai[:], lr[:], ALU.mult))
                        dv(lambda e: e.tensor_tensor(sn[:], mag[:], Aim[:], ALU.mult))
                        dv(lambda e: e.tensor_tensor(ci[:], cs[:], sn[:], ALU.subtract))
                        dv(lambda e: e.tensor_tensor(ci[:], ci[:], den[:], ALU.mult))
                        bbr = gt('bbr', [64, 64, 16], F32, g1); bbi = gt('bbi', [64, 64, 16], F32, g1); tb = gt('tb', [64, 64, 16], F32, g1)
                        crb = cr[:].unsqueeze(2).to_broadcast([64, 64, 16])
                        cib = ci[:].unsqueeze(2).to_broadcast([64, 64, 16])
                        dv(lambda e: e.tensor_tensor(bbr[:], Bre[:], crb, ALU.mult), rd=['Bre'])
                        dv(lambda e: e.tensor_tensor(tb[:], Bim[:], cib, ALU.mult), rd=['Bim'])
                        dv(lambda e: e.tensor_tensor(bbr[:], bbr[:], tb[:], ALU.subtract))
                        dv(lambda e: e.tensor_tensor(bbi[:], Bim[:], crb, ALU.mult))
                        dv(lambda e: e.tensor_tensor(tb[:], Bre[:], cib, ALU.mult))
                        dv(lambda e: e.tensor_tensor(bbi[:], bbi[:], tb[:], ALU.add))
                        pg = cg[0] % 2
                        cg[0] += 1
                        k.op('pe', lambda e: e.transpose(pG[pg][0:64, 0:64], ar[:], identf[0:64, 0:64]), reads=[gk, 'identf'], writes=[('pG', pg)], inc=False)
                        k.op('pe', lambda e: e.transpose(pG[pg][0:64, 64:128], ai[:], identf[0:64, 0:64]), reads=[gk, 'identf'], writes=[('pG', pg)])
                        k.op('act', lambda e: e.copy(lam[:].rearrange("p r g -> p (r g)"), pG[pg][0:64, 0:128]), reads=[('pG', pg)], writes=['lam'])
                        for (srcs, dsts, isC) in (((bbr, bbi), BBp, False), ((Cre, Cim), CCp, True)):
                            for r in range(2):
                                for c0 in (0, 8):
                                    pg = cg[0] % 2
                                    cg[0] += 1
                                    for j in range(8):
                                        cidx = c0 + j
                                        src = srcs[r][:, cidx, :] if isC else srcs[r][:, :, cidx]
                                        k.op('pe', lambda e, src=src, j=j, pg=pg: e.transpose(pG[pg][0:64, j * 64:(j + 1) * 64], src, identf[0:64, 0:64]),
                                             reads=[gk, 'identf', 'Cre', 'Cim'], writes=[('pG', pg)], inc=(j == 7))
                                    dst = dsts[r][:, :, c0:c0 + 8].rearrange("p g c -> p c g")
                                    k.op('act', lambda e, dst=dst, pg=pg: e.copy(dst, pG[pg][0:64, 0:512].rearrange("p (c g) -> p c g", c=8)),
                                         reads=[('pG', pg)], writes=['BBCC'])
                        k.barrier()
                    with ExitStack() as g2:
                        t1 = gt('pw1', [64, 64], F32, g2); t2 = gt('pw2', [64, 64], F32, g2)
                        pk = 'pw'

                        def pl(fn, rd=(), wr=()):
                            k.op('pool', fn, reads=[pk] + list(rd), writes=[pk] + list(wr))
                        pl(lambda e: e.memset(PW[:, 0, 0, :], 1.0))
                        pl(lambda e: e.memset(PW[:, 0, 1, :], 0.0))
                        for kk in range(1, 9):
                            pr, pi_ = PW[:, kk - 1, 0, :], PW[:, kk - 1, 1, :]
                            pl(lambda e, pr=pr: e.tensor_tensor(t1[:], pr, lam[:, 0, :], ALU.mult), rd=['lam'])
                            pl(lambda e, pi_=pi_: e.tensor_tensor(t2[:], pi_, lam[:, 1, :], ALU.mult))
                            pl(lambda e, kk=kk: e.tensor_tensor(PW[:, kk, 0, :], t1[:], t2[:], ALU.subtract))
                            pl(lambda e, pr=pr: e.tensor_tensor(t1[:], pr, lam[:, 1, :], ALU.mult))
                            pl(lambda e, pi_=pi_: e.tensor_tensor(t2[:], pi_, lam[:, 0, :], ALU.mult))
                            pl(lambda e, kk=kk: e.tensor_tensor(PW[:, kk, 1, :], t1[:], t2[:], ALU.add))
                        pl(lambda e: e.tensor_copy(COEF[d][:, 0, :], PW[:, 8, 0, :]), wr=[('COEF', d)])
                        pl(lambda e: e.tensor_scalar(COEF[d][:, 1, :], PW[:, 8, 1, :], -1.0, None, ALU.mult), wr=[('COEF', d)])
                        pl(lambda e: e.tensor_copy(COEF[d][:, 2, :], PW[:, 8, 1, :]), wr=[('COEF', d)])
                        pl(lambda e: e.tensor_copy(COEF[d][:, 3, :], PW[:, 8, 0, :]), wr=[('COEF', d)])
                        for kk in range(9):
                            sc = kk if d == 0 else 8 - kk
                            pl(lambda e, kk=kk, sc=sc: e.tensor_copy(EC[:, :, :, sc], PW[:, kk, :, :]), wr=['EC'])
                            if kk <= 7:
                                sb_ = 7 - kk if d == 0 else kk
                                pl(lambda e, kk=kk, sb_=sb_: e.tensor_copy(EB[:, :, :, sb_], PW[:, kk, :, :]), wr=['EB'])
                        u1 = gt('u1', [64, 64, 16], F32, g2); u2 = gt('u2', [64, 64, 16], F32, g2)
                        PB = [gt(f'PB{r}', [64, 32, 8, 16], BF16, g2) for r in range(2)]
                        Ktap = gt('Ktap', [16, 64, 128], BF16, g2)
                        Ktf = gt('Ktf', [16, 4, 128], F32, g2)
                        bk = 'PBPC'

                        def bd(eng, fn, wr=()):
                            k.op(eng, fn, reads=[bk, 'EB', 'EC', 'BBCC', pk], writes=[bk] + list(wr))

                        def eb(Et, r, j, g0=0, gn=64):
                            return Et[:, r, g0:g0 + gn, j:j + 1].to_broadcast([64, gn, 16])
                        for j in range(9):
                            bd('dve', lambda e, j=j: e.tensor_tensor(u1[:], CCp[0][:], eb(EC, 0, j), ALU.mult))
                            bd('pool', lambda e, j=j: e.tensor_tensor(u2[:], CCp[1][:], eb(EC, 1, j), ALU.mult))
                            bd('dve', lambda e, j=j: e.tensor_tensor(PC[d][0][:, :, j, :], u1[:], u2[:], ALU.subtract), wr=[('PC', d)])
                            bd('dve', lambda e, j=j: e.tensor_tensor(u1[:], CCp[1][:], eb(EC, 0, j), ALU.mult))
                            bd('pool', lambda e, j=j: e.tensor_tensor(u2[:], CCp[0][:], eb(EC, 1, j), ALU.mult))
                            bd('dve', lambda e, j=j: e.tensor_tensor(u1[:], u1[:], u2[:], ALU.add))
                            bd('dve', lambda e, j=j: e.tensor_scalar(PC[d][1][:, :, j, :], u1[:], -1.0, None, ALU.mult), wr=[('PC', d)])
                        for r in range(2):
                            bd('dve', lambda e, r=r: e.tensor_copy(BBb[r][:], BBp[r][:]))
                        for gh in range(2):
                            G0 = gh * 32
                            v1, v2 = u1[:, 0:32, :], u2[:, 0:32, :]
                            for j in range(8):
                                bd('dve', lambda e, j=j: e.tensor_tensor(v1, BBp[0][:, G0:G0 + 32, :], eb(EB, 0, j, G0, 32), ALU.mult))
                                bd('pool', lambda e, j=j: e.tensor_tensor(v2, BBp[1][:, G0:G0 + 32, :], eb(EB, 1, j, G0, 32), ALU.mult))
                                bd('dve', lambda e, j=j: e.tensor_tensor(PB[0][:, :, j, :], v1, v2, ALU.subtract))
                                bd('dve', lambda e, j=j: e.tensor_tensor(v1, BBp[1][:, G0:G0 + 32, :], eb(EB, 0, j, G0, 32), ALU.mult))
                                bd('pool', lambda e, j=j: e.tensor_tensor(v2, BBp[0][:, G0:G0 + 32, :], eb(EB, 1, j, G0, 32), ALU.mult))
                                bd('dve', lambda e, j=j: e.tensor_tensor(PB[1][:, :, j, :], v1, v2, ALU.add))
                            for g0 in range(0, 32, 8):
                                pg = cg[0] % 2
                                cg[0] += 1
                                for gl in range(8):
                                    for r in range(2):
                                        g = g0 + gl
                                        last = (gl == 7 and r == 1)
                                        k.op('pe', lambda e, g=g, r=r, gl=gl, pg=pg: e.transpose(
                                            pGb[pg][:, (gl * 2 + r) * 64:(gl * 2 + r + 1) * 64],
                                            PB[r][:, g, :, :].rearrange("p s c -> p (s c)"), ident[0:64, 0:64]),
                                            reads=[bk, 'ident'], writes=[('pGb', pg)], inc=last)
                                k.op('act', lambda e, g0=g0, pg=pg: e.copy(WinT[:, G0 + g0:G0 + g0 + 8, d, :, :], pGb[pg][:, :].rearrange("p (g r q) -> p g r q", g=8, r=2)),
                                     reads=[('pGb', pg)], writes=['WinT'])
                        ts0 = 0 if d == 0 else 1
                        for g0 in range(0, 64, 4):
                            pg = cg[0] % 2
                            cg[0] += 1
                            for gl in range(4):
                                g = g0 + gl
                                k.op('pe', lambda e, g=g, gl=gl, pg=pg: e.matmul(pG[pg][0:16, gl * 128:(gl + 1) * 128], BBb[0][:, g, :],
                                                                          PC[d][0][:, g, ts0:ts0 + 8, :].rearrange("p j c -> p (j c)"), start=True, stop=False),
                                     reads=[bk, ('PC', d)], writes=[('pG', pg)], inc=False)
                                k.op('pe', lambda e, g=g, gl=gl, pg=pg: e.matmul(pG[pg][0:16, gl * 128:(gl + 1) * 128], BBb[1][:, g, :],
                                                                          PC[d][1][:, g, ts0:ts0 + 8, :].rearrange("p j c -> p (j c)"), start=False, stop=True),
                                     reads=[bk, ('PC', d)], writes=[('pG', pg)], inc=(gl == 3))
                            k.op('act', lambda e, pg=pg: e.copy(Ktf[:], pG[pg][0:16, :].rearrange("p (g n) -> p g n", g=4)),
                                 reads=[('pG', pg), 'Ktf'], writes=['Ktf'])
                            if d == 1:
                                k.op('dve', lambda e, g0=g0: e.tensor_copy(Kb0[:, g0:g0 + 4, :], Ktf[:, :, 112:128]), reads=['Ktf'], writes=['Kb0'])
                            else:
                                k.op('dve', lambda e, g0=g0: e.tensor_tensor(Ktf[:, :, 0:16], Ktf[:, :, 0:16], Kb0[:, g0:g0 + 4, :], ALU.add),
                                     reads=['Ktf', 'Kb0'], writes=['Ktf'])
                                k.op('dve', lambda e, g0=g0: e.tensor_tensor(Kb0[:, g0:g0 + 4, :], Dbc[:, g0:g0 + 4, :],
                                                                            identf[0:16, 0:16].unsqueeze(1).to_broadcast([16, 4, 16]), ALU.mult),
                                     reads=['Dbc', 'identf', 'Kb0', 'Ktf'], writes=['Kb0'])
                                k.op('dve', lambda e, g0=g0: e.tensor_tensor(Ktf[:, :, 0:16], Ktf[:, :, 0:16], Kb0[:, g0:g0 + 4, :], ALU.add),
                                     reads=['Ktf', 'Kb0'], writes=['Ktf'])
                            k.op('dve', lambda e, g0=g0: e.tensor_copy(Ktap[:, g0:g0 + 4, :], Ktf[:]), reads=['Ktf'], writes=['Ktap'])
                        for s in range(8):
                            if d == 0:
                                k.dma(Toep[s * 16:(s + 1) * 16, :, s * 16:128], Ktap[:, :, 0:(8 - s) * 16], reads=['Ktap'], writes=['Toep'])
                            elif s >= 1:
                                k.dma(Toep[s * 16:(s + 1) * 16, :, 0:s * 16], Ktap[:, :, (7 - s) * 16:112], reads=['Ktap'], writes=['Toep'])
                        k.barrier()
        k.barrier()

        u_v = S['u'].rearrange("(n s) f -> n (s f)", s=8)
        tilesA = [(0, 128), (128, 128), (256, 128), (384, 128), (512, 32)]
        with ExitStack() as pa:
            ut = k.sb('A_ut', [128, 8, D], F32, pa)
            ub = k.sb('A_ub', [128, 64, 8, 16], BF16, pa)
            Ub = [k.sb(f'A_Ub{i}', [128, 64, 128], BF16, pa) for i in range(2)]
            pA = [k.ps(f'pA{i}', [128, 8, 128], BF16, pa) for i in range(2)]
            ca = 0
            for it, (c0, m) in enumerate(tilesA):
                k.dma(ut[0:m].rearrange("p s f -> p (s f)"), u_v[c0:c0 + m, :], reads=[('u', t) for t in range(NT)], writes=['A_ut'])
                k.op('dve', lambda e: e.tensor_copy(ub[0:m, :, 0:4, :].rearrange("p g s c -> p s g c"), ut[0:m, 0:4, :].rearrange("p s (g c) -> p s g c", g=64)), reads=['A_ut'], writes=['A_ub0'])
                k.op('pool', lambda e: e.tensor_copy(ub[0:m, :, 4:8, :].rearrange("p g s c -> p s g c"), ut[0:m, 4:8, :].rearrange("p s (g c) -> p s g c", g=64)), reads=['A_ut'], writes=['A_ub1'])
                ubuf = Ub[it % 2]
                for g0 in range(0, 64, 8):
                    pa_ = ca % 2
                    ca += 1
                    for gl in range(8):
                        g = g0 + gl
                        k.op('pe', lambda e, g=g, gl=gl: e.transpose(pA[pa_][:, gl, 0:m], ub[0:m, g, :, :].rearrange("p s c -> p (s c)"), ident[0:m, 0:m]),
                             reads=['A_ub0', 'A_ub1', 'ident'], writes=[('pA', pa_)], inc=(gl == 7))
                    if (g0 // 8) % 2 == 0:
                        k.op('act', lambda e, g0=g0: e.copy(ubuf[:, g0:g0 + 8, 0:m], pA[pa_][:, :, 0:m]), reads=[('pA', pa_)], writes=[('A_Ub', it % 2)])
                    else:
                        k.op('dve', lambda e, g0=g0: e.tensor_copy(ubuf[:, g0:g0 + 8, 0:m], pA[pa_][:, :, 0:m]), reads=[('pA', pa_)], writes=[('A_Ub', it % 2)])
                k.dma(S['U'][:, :, c0:c0 + m], ubuf[:, :, 0:m], reads=[('A_Ub', it % 2)], writes=['U_scr'])
        k.barrier()

        orders = [list(range(NBLK)), [0] + list(range(NBLK - 1, 0, -1))]
        engs = ['dve', 'pool']
        with ExitStack() as pb:
            Ublk = [[k.sb(f'B_U{d}{i}', [128, 64, NB], BF16, pb) for i in range(2)] for d in range(2)]
            Sb = [k.sb(f'B_S{d}', [64, 2, 64, NB], F32, pb) for d in range(2)]
            HS = [k.sb(f'B_HS{d}', [64, 2, 64, NB + 1], F32, pb) for d in range(2)]
            prod = [k.sb(f'B_pr{d}', [64, 2, 2, 64], F32, pb) for d in range(2)]
            hn = [k.sb(f'B_hn{d}', [64, 2, 64], F32, pb) for d in range(2)]
            pS = [[k.ps(f'pS{d}{i}', [128, 512], F32, pb) for i in range(2)] for d in range(2)]
            for d in range(2):
                k.op(engs[d], lambda e, d=d: e.memset(HS[d][:], 0.0), writes=[('HS', d)])
            cs_ = [0, 0]
            for step in range(NBLK):
                for d in range(2):
                    eng = engs[d]
                    blk = orders[d][step]
                    c0 = blk * NB
                    ub_ = Ublk[d][step % 2]
                    ukey = ('B_U', d, step % 2)
                    k.dma(ub_[:], S['U'][:, :, c0:c0 + NB], reads=['U_scr'], writes=[ukey])
                    for g0 in range(0, 64, 8):
                        pi = cs_[d] % 2
                        cs_[d] += 1
                        for gl in range(8):
                            for r in range(2):
                                g = g0 + gl
                                k.op('pe', lambda e, g=g, r=r, gl=gl, d=d, pi=pi, ub_=ub_: e.matmul(
                                    pS[d][pi][0:64, (gl * 2 + r) * NB:(gl * 2 + r + 1) * NB], WinT[:, g, d, r, :], ub_[:, g, :], start=True, stop=True),
                                    reads=['WinT', ukey], writes=[('pS', d, pi)], inc=(gl == 7 and r == 1))
                        k.op('act', lambda e, g0=g0, d=d, pi=pi: e.copy(
                            Sb[d][:, :, g0:g0 + 8, :].rearrange("p r g n -> p g r n"),
                            pS[d][pi][0:64, :].rearrange("p (g r n) -> p g r n", g=8, r=2)),
                            reads=[('pS', d, pi)], writes=[('Sb', d)])
                    hk = ('HS', d)
                    rng = range(NB) if d == 0 else range(NB - 1, -1, -1)
                    for j in rng:
                        cin = j if d == 0 else j + 1
                        cout = j + 1 if d == 0 else j
                        hin = HS[d][:, :, :, cin]
                        k.op(eng, lambda e, hin=hin, d=d: e.tensor_tensor(
                            prod[d][:], COEF[d][:].rearrange("p (a b) g -> p a b g", a=2), hin.unsqueeze(1).to_broadcast([64, 2, 2, 64]), ALU.mult),
                            reads=[hk, ('COEF', d)], writes=[('prod', d)])
                        k.op(eng, lambda e, d=d: e.tensor_tensor(hn[d][:], prod[d][:, :, 0, :], prod[d][:, :, 1, :], ALU.add),
                             reads=[('prod', d)], writes=[('hn', d)])
                        k.op(eng, lambda e, d=d, j=j, cout=cout: e.tensor_tensor(HS[d][:, :, :, cout], hn[d][:], Sb[d][:, :, :, j], ALU.add),
                             reads=[('hn', d), ('Sb', d)], writes=[hk])
                    Hs = S['H0'] if d == 0 else S['H1']
                    if d == 0:
                        k.dma(Hs[:, :, :, c0:c0 + NB], HS[d][:, :, :, 0:NB], reads=[hk], writes=[('Hscr', d)])
                        k.op(eng, lambda e, d=d: e.tensor_copy(HS[d][:, :, :, 0], HS[d][:, :, :, NB]), reads=[hk], writes=[hk])
                    else:
                        k.dma(Hs[:, :, :, c0:c0 + NB], HS[d][:, :, :, 1:NB + 1], reads=[hk], writes=[('Hscr', d)])
                        k.op(eng, lambda e, d=d: e.tensor_copy(HS[d][:, :, :, NB], HS[d][:, :, :, 0]), reads=[hk], writes=[hk])
        k.barrier()

        y_v = S['y'].rearrange("(n s) f -> n (s f)", s=8)
        tilesC = [(i * 64, 64) for i in range(8)] + [(512, 32)]
        with ExitStack() as pc:
            Hc = [[k.sb(f'C_H{d}{i}', [64, 2, 64, 64], BF16, pc) for i in range(1)] for d in range(2)]
            Uc = [k.sb(f'C_U{i}', [128, 64, 64], BF16, pc) for i in range(2)]
            yt = k.sb('C_y', [64, 8, D], F32, pc)
            pY = [k.ps(f'pY{i}', [128, 512], F32, pc) for i in range(4)]
            cy = 0
            for it, (c0, m) in enumerate(tilesC):
                uc_ = Uc[it % 2]
                k.dma(uc_[:, :, 0:m], S['U'][:, :, c0:c0 + m], reads=['U_scr'], writes=[('C_U', it % 2)])
                for d in range(2):
                    Hs = S['H0'] if d == 0 else S['H1']
                    for r in range(2):
                        k.dma(Hc[d][0][:, r, :, 0:m], Hs[:, r, :, c0:c0 + m], reads=[('Hscr', d)], writes=[('C_H', d)], q='pool')
                for g0 in range(0, 64, 4):
                    py = cy % 4
                    cy += 1
                    for gl in range(4):
                        g = g0 + gl
                        o_ = pY[py][0:m, gl * 128:(gl + 1) * 128]
                        k.op('pe', lambda e, g=g, o_=o_: e.matmul(o_, uc_[:, g, 0:m], Toep[:, g, :], start=True, stop=False),
                             reads=[('C_U', it % 2), 'Toep'], writes=[('pY', py)], inc=False)
                        for d in range(2):
                            ws = 1 if d == 0 else 0
                            for r in range(2):
                                last = (d == 1 and r == 1)
                                k.op('pe', lambda e, g=g, o_=o_, d=d, r=r, ws=ws, last=last: e.matmul(
                                    o_, Hc[d][0][:, r, g, 0:m], PC[d][r][:, g, ws:ws + 8, :].rearrange("p j c -> p (j c)"), start=False, stop=last),
                                    reads=[('C_H', d), ('PC', d)], writes=[('pY', py)], inc=(last and gl == 3))
                    dst = yt[0:m, :, g0 * 16:(g0 + 4) * 16].rearrange("p i (g c) -> p g i c", g=4)
                    src = pY[py][0:m, :].rearrange("p (g i c) -> p g i c", g=4, i=8)
                    if (g0 // 4) % 2 == 0:
                        k.op('act', lambda e, dst=dst, src=src: e.copy(dst, src), reads=[('pY', py)], writes=['C_y'])
                    else:
                        k.op('dve', lambda e, dst=dst, src=src: e.tensor_copy(dst, src), reads=[('pY', py)], writes=['C_y'])
                k.dma(y_v[c0:c0 + m, :], yt[0:m].rearrange("p s f -> p (s f)"), reads=['C_y'], writes=['y_scr'])
        k.barrier()


def mla(G, I, S, linear_phase, load_w, make_modnorm, make_residual_epi, simple_load, stream_src, rms_rstd):
    k = G.k
    ident, identf = G.ident, G.identf
    ALLT = list(range(NT))
    XT = list(range(2, NT))
    with ExitStack() as ps:
        wsb = k.sb('mwi', [128, 8, 800], BF16, ps)
        load_w(wsb, I['mla_w_in'][0], D, 'mwi')
        pro = make_modnorm('mli', 1, 0, ps)

        def epi(t, yt, ykeys):
            k.dma(S['P'][t * 128:(t + 1) * 128, :], yt[:], reads=ykeys, writes=[('P', t)])
        def ldx2(t, tile, key):
            k.dma(tile[:], S['x2'][t * 128:(t + 1) * 128, :], reads=[('x2', t)], writes=[key])
        linear_phase('mli', ALLT, D, 800, wsb, 'mwi', ldx2, pro, epi)

    def head_norm_rope(pname, yv, nh, hd, gain, cs, t, keys, stk_tiles, rope):
        sq, ssum = stk_tiles
        k.op('dve', lambda e: e.tensor_tensor(sq[:], yv, yv, ALU.mult), reads=keys, writes=[(pname, 'hsq')])
        k.op('dve', lambda e: e.reduce_sum(ssum[:, 0, :], sq[:], AX.X), reads=[(pname, 'hsq')], writes=[(pname, 'hss')])
        k.op('act', lambda e: e.activation(ssum[:, 1, :], ssum[:, 0, :], AF.Sqrt, bias=EPS, scale=1.0 / 96), reads=[(pname, 'hss')], writes=[(pname, 'hss')])
        k.op('dve', lambda e: e.reciprocal(ssum[:, 1, :], ssum[:, 1, :]), reads=[(pname, 'hss')], writes=[(pname, 'hss')])
        k.op('dve', lambda e: e.tensor_tensor(yv, yv, ssum[:, 1, :].unsqueeze(2).to_broadcast([128, nh, 96]), ALU.mult),
             reads=keys + [(pname, 'hss')], writes=keys)
        k.op('pool', lambda e: e.tensor_tensor(yv, yv, gain[:].unsqueeze(1).to_broadcast([128, nh, 96]), ALU.mult),
             reads=keys + ['gains'], writes=keys)
        if rope:
            x1 = yv[:, :, 64:96].rearrange("p h (a f e) -> p h a f e", a=2, f=2)
            rt = sq[:, :, 0:64].rearrange("p h (w a e) -> p h w a e", w=4, a=2)
            c_b = cs[:, 0:16].rearrange("p (a e) -> p a e", a=2).unsqueeze(1).to_broadcast([128, nh, 2, 8])
            s_b = cs[:, 16:32].rearrange("p (a e) -> p a e", a=2).unsqueeze(1).to_broadcast([128, nh, 2, 8])
            a1 = x1[:, :, :, 0, :]
            a2 = x1[:, :, :, 1, :]
            rk = (pname, 'ropetmp')
            k.op('dve', lambda e: e.tensor_tensor(rt[:, :, 0], a1, c_b, ALU.mult), reads=keys + ['ropecs', (pname, 'hsq')], writes=[rk])
            k.op('dve', lambda e: e.tensor_tensor(rt[:, :, 1], a2, s_b, ALU.mult), reads=keys + ['ropecs'], writes=[rk])
            k.op('dve', lambda e: e.tensor_tensor(rt[:, :, 2], a2, c_b, ALU.mult), reads=keys + ['ropecs'], writes=[rk])
            k.op('dve', lambda e: e.tensor_tensor(rt[:, :, 3], a1, s_b, ALU.mult), reads=keys + ['ropecs'], writes=[rk])
            k.op('dve', lambda e: e.tensor_tensor(a1, rt[:, :, 0], rt[:, :, 1], ALU.subtract), reads=[rk], writes=keys)
            k.op('dve', lambda e: e.tensor_tensor(a2, rt[:, :, 2], rt[:, :, 3], ALU.add), reads=[rk], writes=keys)

    with ExitStack() as gs:
        gq = k.sb('g_q', [128, 96], F32, gs)
        gk_ = k.sb('g_k', [128, 96], F32, gs)
        gqa = k.sb('g_qa', [128, 512], F32, gs)
        gkva = k.sb('g_kva', [128, 256], F32, gs)
        k.dma(gq[:], I['mla_q_norm'][0].partition_broadcast(128), writes=['gains'])
        k.dma(gk_[:], I['mla_k_norm'][0].partition_broadcast(128), writes=['gains'])
        k.dma(gqa[:], I['mla_q_a_norm'][0].partition_broadcast(128), writes=['gains'])
        k.dma(gkva[:], I['mla_kv_a_norm'][0].partition_broadcast(128), writes=['gains'])
        csb = [k.sb(f'ropecs{i}', [128, 32], F32, gs) for i in range(2)]

        with ExitStack() as ps:
            wsb = k.sb('mwq', [128, 4, 1536], BF16, ps)
            load_w(wsb, I['mla_w_q_b'][0], 512, 'mwq')
            sq = k.sb('mq_sq', [128, 512], F32, ps)
            ss = k.sb('mq_ss', [128, 2], F32, ps)
            hsq = k.sb('mq_hsq', [128, 16, 96], F32, ps)
            hss = k.sb('mq_hss', [128, 2, 16], F32, ps)
            qb = [k.sb(f'mq_qb{i}', [128, 1536], BF16, ps) for i in range(2)]
            cnt = [0]

            def ld(t, tile, key):
                k.dma(tile[:], S['P'][t * 128:(t + 1) * 128, 0:512], reads=[('P', t)], writes=[key])

            def pro(t, xt, xb, ikey, okey):
                rms_rstd(xt[:], 512, ss, ikey, 'mq', sq[:])
                k.op('dve', lambda e: e.scalar_tensor_tensor(xb[:], xt[:], ss[:, 1:2], gqa[:], ALU.mult, ALU.mult),
                     reads=[ikey, ('mq', 'ss1'), 'gains'], writes=[okey])

            def epi(t, yt, ykeys):
                b = cnt[0] % 2
                cnt[0] += 1
                cs = csb[b]
                k.dma(cs[:], I['rope_cs'][(t - 2) * 128:(t - 1) * 128, :], writes=['ropecs'])
                head_norm_rope('mq', yt[:].rearrange("p (h e) -> p h e", h=16), 16, 96, gq, cs, t, ykeys, (hsq, hss), True)
                k.op('act', lambda e: e.copy(qb[b][:], yt[:]), reads=ykeys, writes=[('mq_qb', b)])
                k.dma(S['q'][(t - 2) * 128:(t - 1) * 128, :], qb[b][:], reads=[('mq_qb', b)], writes=['q_scr'])
            linear_phase('mq', XT, 512, 1536, wsb, 'mwq', ld, pro, epi)

        with ExitStack() as ps:
            wsb = k.sb('mwk', [128, 2, 2048], BF16, ps)
            load_w(wsb, I['mla_w_kv_b'][0], 256, 'mwk')
            sq = k.sb('mk_sq', [128, 256], F32, ps)
            ss = k.sb('mk_ss', [128, 2], F32, ps)
            hsq = k.sb('mk_hsq', [128, 16, 96], F32, ps)
            hss = k.sb('mk_hss', [128, 2, 16], F32, ps)
            kf = k.sb('mk_kf', [128, 16, 96], F32, ps)
            kpe = [k.sb(f'mk_kpe{i}', [128, 32], F32, ps) for i in range(2)]
            kb = [k.sb(f'mk_kb{i}', [128, 16, 96], BF16, ps) for i in range(2)]
            vb = [k.sb(f'mk_vb{i}', [128, 16, 65], BF16, ps) for i in range(2)]
            for i in range(2):
                k.op('pool', lambda e, i=i: e.memset(vb[i][:], 1.0), writes=[('mk_vb', i)])
            cnt = [0]

            def ld(t, tile, key):
                k.dma(tile[:], S['P'][t * 128:(t + 1) * 128, 512:768], reads=[('P', t)], writes=[key])

            def pro(t, xt, xb, ikey, okey):
                rms_rstd(xt[:], 256, ss, ikey, 'mk', sq[:])
                k.op('dve', lambda e: e.scalar_tensor_tensor(xb[:], xt[:], ss[:, 1:2], gkva[:], ALU.mult, ALU.mult),
                     reads=[ikey, ('mk', 'ss1'), 'gains'], writes=[okey])

            def epi(t, yt, ykeys):
                b = cnt[0] % 2
                cnt[0] += 1
                cs = csb[b]
                if t >= 2:
                    k.dma(cs[:], I['rope_cs'][(t - 2) * 128:(t - 1) * 128, :], writes=['ropecs'])
                k.dma(kpe[b][:], S['P'][t * 128:(t + 1) * 128, 768:800], reads=[('P', t)], writes=[('mk_kpe', b)])
                yv = yt[:].rearrange("p (h e) -> p h e", h=16)
                k.op('act', lambda e: e.copy(kf[:, :, 0:64], yv[:, :, 0:64]), reads=ykeys, writes=['mk_kf'])
                k.op('pool', lambda e: e.tensor_copy(kf[:, :, 64:96], kpe[b][:].unsqueeze(1).to_broadcast([128, 16, 32])),
                     reads=[('mk_kpe', b)], writes=['mk_kf'])
                head_norm_rope('mk', kf[:], 16, 96, gk_, cs, t, ['mk_kf'], (hsq, hss), t >= 2)
                k.op('act', lambda e: e.copy(kb[b][:], kf[:]), reads=['mk_kf'], writes=[('mk_kb', b)])
                k.op('dve', lambda e: e.tensor_copy(vb[b][:, :, 0:64], yv[:, :, 64:128]), reads=ykeys, writes=[('mk_vb', b)])
                k.dma(S['k'][t * 128:(t + 1) * 128, :], kb[b][:].rearrange("p h e -> p (h e)"), reads=[('mk_kb', b)], writes=['k_scr'])
                k.dma(S['v'][t * 128:(t + 1) * 128, :, :], vb[b][:], reads=[('mk_vb', b)], writes=['v_scr'])
            linear_phase('mk', ALLT, 256, 2048, wsb, 'mwk', ld, pro, epi)

    SCALE = 96 ** -0.5
    with ExitStack() as at:
        KT = [k.sb(f'at_KT{i}', [96, NT, 128], BF16, at) for i in range(2)]
        QT = [k.sb(f'at_QT{i}', [96, 32, 128], BF16, at) for i in range(2)]
        Vh = [k.sb(f'at_V{i}', [128, NT, 65], BF16, at) for i in range(2)]
        ktm = [k.sb(f'at_ktm{i}', [128, NT, 96], BF16, at) for i in range(2)]
        qtm = [k.sb(f'at_qtm{i}', [128, 32, 96], BF16, at) for i in range(2)]
        PT = [k.sb(f'at_PT{i}', [128, 512], BF16, at) for i in range(3)]
        OT = [k.sb(f'at_OT{i}', [65, 512], F32, at) for i in range(2)]
        On = [k.sb(f'at_On{i}', [128, 4, 65], F32, at) for i in range(2)]
        Ob = [k.sb(f'at_Ob{i}', [128, 4, 64], BF16, at) for i in range(2)]
        rin = [k.sb(f'at_ri{i}', [128, 4, 1], F32, at) for i in range(2)]
        pTr = [k.ps(f'at_pTr{i}', [128, 8, 128], BF16, at) for i in range(2)]
        pSc = [k.ps(f'at_pS{i}', [128, 512], F32, at) for i in range(3)]
        pO = [k.ps(f'at_pO{i}', [128, 512], F32, at) for i in range(2)]
        pF = k.ps('at_pF', [128, 4, 65], F32, at)
        ctr = 0
        def prep(h):
            nonlocal ctr
            hb = h % 2
            k.dma(ktm[hb][:], S['k'].rearrange("(t p) (h e) -> p t h e", p=128, h=16)[:, :, h, :], reads=['k_scr'], writes=[('ktm', hb)])
            k.dma(qtm[hb][:], S['q'].rearrange("(t p) (h e) -> p t h e", p=128, h=16)[:, :, h, :], reads=['q_scr'], writes=[('qtm', hb)])
            k.dma(Vh[hb][:], S['v'].rearrange("(t p) h e -> p t h e", p=128)[:, :, h, :], reads=['v_scr'], writes=[('Vh', hb)])
            for (srct, dstT, n, skey, dkey) in ((ktm[hb], KT[hb], NT, ('ktm', hb), ('KT', hb)), (qtm[hb], QT[hb], 32, ('qtm', hb), ('QT', hb))):
                for t0 in range(0, n, 8):
                    tn = min(8, n - t0)
                    pi = ctr % 2
                    ctr += 1
                    for j in range(tn):
                        k.op('pe', lambda e, j=j, t0=t0, srct=srct, pi=pi: e.transpose(pTr[pi][0:96, j, :], srct[:, t0 + j, :], ident[:]),
                             reads=[skey, 'ident'], writes=[('pTr', pi)], inc=(j == tn - 1))
                    k.op('dve', lambda e, t0=t0, tn=tn, dstT=dstT, pi=pi: e.tensor_copy(dstT[:, t0:t0 + tn, :], pTr[pi][0:96, 0:tn, :]),
                         reads=[('pTr', pi)], writes=[dkey])
        prep(0)
        for h in range(16):
            hb = h % 2
            if h + 1 < 16:
                prep(h + 1)
            items = [(qb_, kt) for qb_ in range(8) for kt in range(NT)]
            base = h * len(items)

            def issue_S(i):
                qb_, kt = items[i]
                si = (base + i) % 3
                qrhs = QT[hb][:, qb_ * 4:(qb_ + 1) * 4, :].rearrange("p t n -> p (t n)")
                k.op('pe', lambda e: e.matmul(pSc[si][:], KT[hb][:, kt, :], qrhs, start=True, stop=True),
                     reads=[('KT', hb), ('QT', hb)], writes=[('pSc', si)])
            issue_S(0)
            issue_S(1)
            for i, (qb_, kt) in enumerate(items):
                si = (base + i) % 3
                po = (h * 8 + qb_) % 2
                k.op('act', lambda e: e.activation(PT[si][:], pSc[si][:], AF.Exp, scale=SCALE),
                     reads=[('pSc', si)], writes=[('PT', si)])
                if i + 2 < len(items):
                    issue_S(i + 2)
                k.op('pe', lambda e: e.matmul(pO[po][0:65, :], Vh[hb][:, kt, :], PT[si][:], start=(kt == 0), stop=(kt == NT - 1)),
                     reads=[('Vh', hb), ('PT', si)], writes=[('pO', po)], inc=(kt == NT - 1))
                if kt != NT - 1:
                    continue
                k.op('dve', lambda e: e.tensor_copy(OT[po][:], pO[po][0:65, :]), reads=[('pO', po)], writes=[('OT', po)])
                for j in range(4):
                    k.op('pe', lambda e, j=j: e.transpose(pF[:, j, :], OT[po][:, j * 128:(j + 1) * 128], identf[0:65, 0:65]),
                         reads=[('OT', po), 'identf'], writes=['pF'], inc=(j == 3))
                k.op('dve', lambda e: e.tensor_copy(On[po][:], pF[:]), reads=['pF'], writes=[('On', po)])
                k.op('dve', lambda e: e.reciprocal(rin[po][:], On[po][:, :, 64:65]), reads=[('On', po)], writes=[('rin', po)])
                k.op('pool', lambda e: e.tensor_tensor(Ob[po][:], On[po][:, :, 0:64], rin[po][:].to_broadcast([128, 4, 64]), ALU.mult),
                     reads=[('On', po), ('rin', po)], writes=[('Ob', po)])
                k.dma(S['att'].rearrange("(t p) (h e) -> p t h e", p=128, h=16)[:, qb_ * 4:(qb_ + 1) * 4, h, :], Ob[po][:],
                      reads=[('Ob', po)], writes=['att_scr'])
    k.barrier()
    with ExitStack() as ps:
        wsb = k.sb('mwo', [128, 8, D], BF16, ps)
        load_w(wsb, I['mla_w_o'][0], D, 'mwo')
        epi = make_residual_epi('mo', stream_src('x2'), 2, lambda t: S['x3'][(t - 2) * 128:(t - 1) * 128, :], ps, dkey='x3', rkey='x2')

        def ld(t, tile, key):
            k.dma(tile[:], S['att'][(t - 2) * 128:(t - 1) * 128, :], reads=['att_scr'], writes=[key])
        linear_phase('mo', XT, D, D, wsb, 'mwo', ld, None, epi, src_dt=BF16)


def rope_table():
    rows = SEQ // 64
    row = np.repeat(np.arange(rows, dtype=np.float32), 64)
    col = np.tile(np.arange(64, dtype=np.float32), rows)
    inv = (np.float32(10000.0) ** (-np.arange(0, 16, 2, dtype=np.float32) / np.float32(16))).astype(np.float32)
    ang = np.stack([row[:, None] * inv, col[:, None] * inv], axis=1).astype(np.float32)
    return np.concatenate([np.cos(ang).reshape(SEQ, 16), np.sin(ang).reshape(SEQ, 16)], axis=1).astype(np.float32)


_NC_CACHE = {}


def kernel(**inputs):
    n = 8
    if 'nc' not in _NC_CACHE:
        _NC_CACHE['nc'] = build()
    nc = _NC_CACHE['nc']
    shared = {kk: np.ascontiguousarray(np.asarray(v, dtype=np.float32)) for kk, v in inputs.items() if kk not in ('x', 'c', 'ctx')}
    rope = rope_table()
    in_maps = []
    for b in range(n):
        m = dict(shared)
        m['x'] = np.ascontiguousarray(np.asarray(inputs['x'][b], dtype=np.float32))
        m['c'] = np.ascontiguousarray(np.asarray(inputs['c'][b], dtype=np.float32))
        m['ctx'] = np.ascontiguousarray(np.asarray(inputs['ctx'][b], dtype=np.float32))
        m['rope_cs'] = rope
        in_maps.append(m)
    res = run_bass_kernel_spmd(nc, in_maps, core_ids=list(range(n)))
    return np.stack([np.asarray(r['out'], dtype=np.float32) for r in res.results], axis=0)
```
